# Optimizing a Trainium2 kernel written in Bass

```python
import math
import jax, jax.numpy as jnp
from jax import lax
import numpy as np

D_MODEL = 4096
BATCH = 1
SEQ = 8192
DEPTH = 2

N_A_LAYERS = DEPTH // 2
N_B_LAYERS = DEPTH - N_A_LAYERS
EPS = 1e-6

SSM_EXPAND = 2
D_INNER = SSM_EXPAND * D_MODEL
SSM_HEAD_DIM = 64
SSM_HEADS = D_INNER // SSM_HEAD_DIM
SSM_GROUPS = 8
SSM_HEADS_PER_GROUP = SSM_HEADS // SSM_GROUPS
D_STATE = 128
SSM_CONV = 4
SSM_CHUNK = 128
SSM_CONV_DIM = D_INNER + 2 * SSM_GROUPS * D_STATE
SSM_IN_DIM = D_INNER + SSM_CONV_DIM + SSM_HEADS
SSM_NORM_EPS = 1e-5

ATTN_HEAD_DIM = 64
N_Q_HEADS = D_MODEL // ATTN_HEAD_DIM
N_KV_HEADS = 8
Q_PER_KV = N_Q_HEADS // N_KV_HEADS
Q_DIM = N_Q_HEADS * ATTN_HEAD_DIM
KV_DIM = N_KV_HEADS * ATTN_HEAD_DIM
WINDOW = 128
ATTN_BLOCK = 128

N_BUCKETS = 32
MAX_DISTANCE = 128

D_FF = 256 * ((8 * D_MODEL // 3 + 255) // 256)
FFN_CONV = 3

kernel_name = 'hybrid_ssd_swa_sink_yoco_block'


def rmsnorm(x, g, eps=EPS):
    xf = x.astype(jnp.float32)
    y = xf * lax.rsqrt(jnp.mean(xf * xf, axis=-1, keepdims=True) + eps)
    return (y * g.astype(jnp.float32)).astype(x.dtype)


def causal_dwconv(x, w, b):
    k_width, seq = w.shape[0], x.shape[1]
    xp = jnp.pad(x, ((0, 0), (k_width - 1, 0), (0, 0)))
    out = b + xp[:, 0:seq] * w[0]
    for k in range(1, k_width):
        out = out + xp[:, k:k + seq] * w[k]
    return out


def ssd_chunked(x, a, bm, cm):
    bsz, seq = x.shape[0], x.shape[1]
    n_chunks = seq // SSM_CHUNK
    x_c = x.reshape(bsz, n_chunks, SSM_CHUNK, SSM_GROUPS, SSM_HEADS_PER_GROUP, SSM_HEAD_DIM).transpose(1, 0, 2, 3, 4, 5)
    a_c = a.reshape(bsz, n_chunks, SSM_CHUNK, SSM_GROUPS, SSM_HEADS_PER_GROUP).transpose(1, 0, 3, 4, 2)
    b_c = bm.reshape(bsz, n_chunks, SSM_CHUNK, SSM_GROUPS, D_STATE).transpose(1, 0, 2, 3, 4)
    c_c = cm.reshape(bsz, n_chunks, SSM_CHUNK, SSM_GROUPS, D_STATE).transpose(1, 0, 2, 3, 4)
    tril = jnp.tril(jnp.ones((SSM_CHUNK, SSM_CHUNK), dtype=bool))

    def step(state, inp):
        xc, ac, bc, cc = inp
        a_cum = jnp.cumsum(ac, axis=-1)
        seg = a_cum[..., :, None] - a_cum[..., None, :]
        decay = jnp.exp(jnp.where(tril, seg, -jnp.inf))
        cb = jnp.einsum('blgn,bsgn->bgls', cc, bc)
        y_diag = jnp.einsum('bgrls,bsgrp->blgrp', cb[:, :, None] * decay, xc)
        y_off = jnp.einsum('blgn,bgrpn->blgrp', cc, state) * jnp.exp(a_cum).transpose(0, 3, 1, 2)[..., None]
        decay_to_end = jnp.exp(a_cum[..., -1:] - a_cum)
        new_state = state * jnp.exp(a_cum[..., -1])[..., None, None] + jnp.einsum('bsgn,bgrs,bsgrp->bgrpn', bc, decay_to_end, xc)
        return new_state, y_diag + y_off

    state0 = jnp.zeros((bsz, SSM_GROUPS, SSM_HEADS_PER_GROUP, SSM_HEAD_DIM, D_STATE), jnp.float32)
    _, ys = lax.scan(step, state0, (x_c, a_c, b_c, c_c))
    return ys.transpose(1, 0, 2, 3, 4, 5).reshape(bsz, seq, SSM_GROUPS, SSM_HEADS_PER_GROUP, SSM_HEAD_DIM)


def mamba2_mixer(u, w_in, conv_w, conv_b, dt_bias, a_log, d_skip, g_norm, w_out):
    bsz, seq, _ = u.shape
    zxbcdt = u @ w_in
    z, xbc, dt = jnp.split(zxbcdt, [D_INNER, D_INNER + SSM_CONV_DIM], axis=-1)
    xbc = jax.nn.silu(causal_dwconv(xbc, conv_w, conv_b))
    xs, bm, cm = jnp.split(xbc, [D_INNER, D_INNER + SSM_GROUPS * D_STATE], axis=-1)
    dt = jax.nn.softplus(dt.astype(jnp.float32) + dt_bias.astype(jnp.float32))
    a = -jnp.exp(a_log.astype(jnp.float32))
    dt = dt.reshape(bsz, seq, SSM_GROUPS, SSM_HEADS_PER_GROUP)
    x_h = xs.astype(jnp.float32).reshape(bsz, seq, SSM_GROUPS, SSM_HEADS_PER_GROUP, SSM_HEAD_DIM)
    y = ssd_chunked(x_h * dt[..., None], dt * a.reshape(SSM_GROUPS, SSM_HEADS_PER_GROUP),
                    bm.astype(jnp.float32).reshape(bsz, seq, SSM_GROUPS, D_STATE),
                    cm.astype(jnp.float32).reshape(bsz, seq, SSM_GROUPS, D_STATE))
    y = y + d_skip.astype(jnp.float32).reshape(SSM_GROUPS, SSM_HEADS_PER_GROUP, 1) * x_h
    y = y.reshape(bsz, seq, D_INNER) * jax.nn.silu(z.astype(jnp.float32))
    yg = y.reshape(bsz, seq, SSM_GROUPS, D_INNER // SSM_GROUPS)
    yg = yg * lax.rsqrt(jnp.mean(yg * yg, axis=-1, keepdims=True) + SSM_NORM_EPS)
    y = yg.reshape(bsz, seq, D_INNER) * g_norm.astype(jnp.float32)
    return y.astype(u.dtype) @ w_out


def t5_bucket(rel):
    max_exact = N_BUCKETS // 2
    relf = jnp.maximum(rel, 1).astype(jnp.float32)
    large = max_exact + (jnp.log(relf / max_exact) / math.log(MAX_DISTANCE / max_exact) * (N_BUCKETS - max_exact)).astype(jnp.int32)
    large = jnp.minimum(large, N_BUCKETS - 1)
    return jnp.where(rel < max_exact, rel, large)


def shared_kv(h, kv_norm, w_kv, b_kv):
    bsz, seq, _ = h.shape
    kv = rmsnorm(h, kv_norm) @ w_kv + b_kv
    k, v = jnp.split(kv, 2, axis=-1)
    return (k.reshape(bsz, seq, N_KV_HEADS, ATTN_HEAD_DIM), v.reshape(bsz, seq, N_KV_HEADS, ATTN_HEAD_DIM))


def swa_sink_attention(u, k_sh, v_sh, w_q, b_q, sinks, rel_bias, w_o, b_o):
    bsz, seq, _ = u.shape
    nb = seq // ATTN_BLOCK
    q = (u @ w_q + b_q).reshape(bsz, nb, ATTN_BLOCK, N_KV_HEADS, Q_PER_KV, ATTN_HEAD_DIM)
    pad = ((0, 0), (ATTN_BLOCK, 0), (0, 0), (0, 0))
    kb = jnp.pad(k_sh, pad).reshape(bsz, nb + 1, ATTN_BLOCK, N_KV_HEADS, ATTN_HEAD_DIM)
    vb = jnp.pad(v_sh, pad).reshape(bsz, nb + 1, ATTN_BLOCK, N_KV_HEADS, ATTN_HEAD_DIM)
    k2 = jnp.concatenate([kb[:, :-1], kb[:, 1:]], axis=2)
    v2 = jnp.concatenate([vb[:, :-1], vb[:, 1:]], axis=2)
    s = jnp.einsum('bnqgrd,bnkgd->bngrqk', q, k2).astype(jnp.float32) * (ATTN_HEAD_DIM ** -0.5)
    q_idx = jnp.arange(ATTN_BLOCK)
    k_idx = jnp.arange(2 * ATTN_BLOCK)
    rel = q_idx[:, None] - k_idx[None, :] + ATTN_BLOCK
    bias = rel_bias.astype(jnp.float32)[t5_bucket(jnp.maximum(rel, 0))]
    bias = bias.transpose(2, 0, 1).reshape(N_KV_HEADS, Q_PER_KV, ATTN_BLOCK, 2 * ATTN_BLOCK)
    key_pos = jnp.arange(nb)[:, None] * ATTN_BLOCK - ATTN_BLOCK + k_idx[None, :]
    mask = (rel >= 0)[None] & (rel < WINDOW)[None] & (key_pos >= 0)[:, None, :]
    s = jnp.where(mask[None, :, None, None], s + bias, -jnp.inf)
    sink = sinks.astype(jnp.float32).reshape(N_KV_HEADS, Q_PER_KV)[..., None, None]
    m = jnp.maximum(jnp.max(s, axis=-1, keepdims=True), sink)
    p = jnp.exp(s - m)
    p = p / (jnp.sum(p, axis=-1, keepdims=True) + jnp.exp(sink - m))
    o = jnp.einsum('bngrqk,bnkgd->bnqgrd', p.astype(v2.dtype), v2).reshape(bsz, seq, Q_DIM)
    return o @ w_o + b_o


def conv_ffn(u, w_up, conv_w, conv_b, w_down):
    gu = causal_dwconv(u @ w_up, conv_w, conv_b)
    g, v = jnp.split(gu, 2, axis=-1)
    return (jax.nn.silu(g) * v) @ w_down


def setup_inputs(seed: int = 0) -> dict:
    key = jax.random.key(seed)
    ks = jax.random.split(key, 26)

    def nrm(k, shape, scale):
        return jax.random.normal(k, shape, jnp.float32) * scale

    def gain(k, shape):
        return 1.0 + nrm(k, shape, 0.05)

    dt = jnp.exp(jax.random.uniform(ks[10], (N_A_LAYERS, SSM_HEADS), jnp.float32, math.log(1e-3), math.log(1e-1)))
    return {
        'x': nrm(ks[0], (BATCH, SEQ, D_MODEL), 1.0),
        'norm_mix_pre': gain(ks[1], (DEPTH, D_MODEL)),
        'norm_mix_post': gain(ks[2], (DEPTH, D_MODEL)),
        'norm_ffn_pre': gain(ks[3], (DEPTH, D_MODEL)),
        'norm_ffn_post': gain(ks[4], (DEPTH, D_MODEL)),
        'ssm_w_in': nrm(ks[5], (N_A_LAYERS, D_MODEL, SSM_IN_DIM), D_MODEL ** -0.5),
        'ssm_conv_w': nrm(ks[6], (N_A_LAYERS, SSM_CONV, SSM_CONV_DIM), SSM_CONV ** -0.5),
        'ssm_conv_b': nrm(ks[7], (N_A_LAYERS, SSM_CONV_DIM), 0.02),
        'ssm_dt_bias': dt + jnp.log(-jnp.expm1(-dt)),
        'ssm_a_log': jnp.log(jax.random.uniform(ks[8], (N_A_LAYERS, SSM_HEADS), jnp.float32, 1.0, 16.0)),
        'ssm_d': gain(ks[9], (N_A_LAYERS, SSM_HEADS)),
        'ssm_norm': gain(ks[11], (N_A_LAYERS, D_INNER)),
        'ssm_w_out': nrm(ks[12], (N_A_LAYERS, D_INNER, D_MODEL), D_INNER ** -0.5),
        'kv_norm': gain(ks[13], (D_MODEL,)),
        'w_kv': nrm(ks[14], (D_MODEL, 2 * KV_DIM), D_MODEL ** -0.5),
        'b_kv': nrm(ks[15], (2 * KV_DIM,), 0.02),
        'attn_w_q': nrm(ks[16], (N_B_LAYERS, D_MODEL, Q_DIM), D_MODEL ** -0.5),
        'attn_b_q': nrm(ks[17], (N_B_LAYERS, Q_DIM), 0.02),
        'attn_sinks': nrm(ks[18], (N_B_LAYERS, N_Q_HEADS), 0.5),
        'attn_w_o': nrm(ks[19], (N_B_LAYERS, Q_DIM, D_MODEL), Q_DIM ** -0.5),
        'attn_b_o': nrm(ks[20], (N_B_LAYERS, D_MODEL), 0.02),
        'rel_bias': nrm(ks[21], (N_BUCKETS, N_Q_HEADS), 0.5),
        'ffn_w_up': nrm(ks[22], (DEPTH, D_MODEL, 2 * D_FF), D_MODEL ** -0.5),
        'ffn_conv_w': nrm(ks[23], (DEPTH, FFN_CONV, 2 * D_FF), FFN_CONV ** -0.5),
        'ffn_conv_b': nrm(ks[24], (DEPTH, 2 * D_FF), 0.02),
        'ffn_w_down': nrm(ks[25], (DEPTH, D_FF, D_MODEL), D_FF ** -0.5),
    }


def reference(x, norm_mix_pre, norm_mix_post, norm_ffn_pre, norm_ffn_post,
              ssm_w_in, ssm_conv_w, ssm_conv_b, ssm_dt_bias, ssm_a_log, ssm_d, ssm_norm, ssm_w_out,
              kv_norm, w_kv, b_kv,
              attn_w_q, attn_b_q, attn_sinks, attn_w_o, attn_b_o, rel_bias,
              ffn_w_up, ffn_conv_w, ffn_conv_b, ffn_w_down):
    h = x
    k_sh, v_sh = None, None
    for i in range(DEPTH):
        if i == N_A_LAYERS:
            k_sh, v_sh = shared_kv(h, kv_norm, w_kv, b_kv)
        u = rmsnorm(h, norm_mix_pre[i])
        if i < N_A_LAYERS:
            mix = mamba2_mixer(u, ssm_w_in[i], ssm_conv_w[i], ssm_conv_b[i], ssm_dt_bias[i],
                               ssm_a_log[i], ssm_d[i], ssm_norm[i], ssm_w_out[i])
        else:
            j = i - N_A_LAYERS
            mix = swa_sink_attention(u, k_sh, v_sh, attn_w_q[j], attn_b_q[j], attn_sinks[j],
                                     rel_bias, attn_w_o[j], attn_b_o[j])
        h = h + rmsnorm(mix, norm_mix_post[i])
        f = conv_ffn(rmsnorm(h, norm_ffn_pre[i]), ffn_w_up[i], ffn_conv_w[i], ffn_conv_b[i], ffn_w_down[i])
        h = h + rmsnorm(f, norm_ffn_post[i])
    return h
```

```python
import numpy as np
import concourse.bass as bass
import concourse.mybir as mybir
from concourse.bass_utils import run_bass_kernel_spmd

F32 = mybir.dt.float32
BF16 = mybir.dt.bfloat16
AF = mybir.ActivationFunctionType
ALU = mybir.AluOpType
AX = mybir.AxisListType

import os as _osx
SSDL = int(_osx.environ.get('SSDL', '9'))
DTL = int(_osx.environ.get('DTL', '9'))
DBG = 9
T = 256
CH = 128
HALO3 = 3
TH = T + HALO3
NEG = -30000.0


class Cfg:
    def __init__(self, D=4096, G=8, R=16, DFF=11008, KVH=8, QPK=8, OWN=4, NC=8):
        self.D, self.G, self.R, self.DFF, self.KVH, self.QPK, self.OWN, self.NC = D, G, R, DFF, KVH, QPK, OWN, NC
        self.KD = D // 128
        self.H = G * R
        self.DI = self.H * 64
        self.RB = R * 64 // 128
        self.GN = G * 128
        self.CD = self.DI + 2 * self.GN
        self.IN = self.DI + self.CD + self.H
        self.CB = self.CD // 128
        self.FB = DFF // 128
        self.QH = KVH * QPK
        self.KVD = KVH * 64
        self.NT = OWN + 1
        self.WIN = self.NT * T
        self.SEQ = NC * OWN * T
        self.HH = min(R, 8)
        self.NHALF = R // self.HH
        o = 0
        self.cp = {}
        for name, n in [("nmp0", self.KD), ("nmp1", self.KD), ("nmq0", self.KD), ("nmq1", self.KD),
                        ("nfp0", self.KD), ("nfp1", self.KD), ("nfq0", self.KD), ("nfq1", self.KD),
                        ("kvn", self.KD), ("scw", 4 * self.CB), ("scb", self.CB),
                        ("fcw0", 3 * 2 * self.FB), ("fcw1", 3 * 2 * self.FB), ("fcb0", 2 * self.FB),
                        ("fcb1", 2 * self.FB), ("gn", self.DI // 128), ("dc", self.DI // 128),
                        ("bq", self.KD), ("bo", self.KD), ("bk", self.KVH)]:
            self.cp[name] = (o, n)
            o += n
        self.NCP = o
        o = 0
        self.cr = {}
        for name, n in [("dtb", self.H), ("alog", self.H), ("bv", self.KVD), ("sink", self.QH)]:
            self.cr[name] = (o, n)
            o += n
        self.NCR = o


def _region(ap):
    t = ap.tensor
    es = mybir.dt.size(ap.dtype)
    pat = [tuple(x) for x in ap.ap]
    if "DRam" in type(t).__name__:
        ext = sum((c - 1) * abs(s) for s, c in pat) + 1
        return (t.name, ap.offset * es, (ap.offset + ext) * es, 0, 1)
    fsz = 1
    for d in list(t.shape)[1:]:
        fsz *= d
    p0 = ap.offset // fsz
    f0 = ap.offset % fsz
    ext = sum((c - 1) * abs(s) for s, c in pat[1:]) + 1
    lo, hi = f0 * es, (f0 + ext) * es
    if t.name == "P":
        lo = lo // 2048 * 2048
        hi = (hi + 2047) // 2048 * 2048
        return (t.name, lo, hi, 0, 128)
    return (t.name, lo, hi, p0, p0 + pat[0][1])


class _Rec:
    def __init__(self):
        self.calls = []

    def __getattr__(self, name):
        def f(*a, **kw):
            self.calls.append((name, a, kw))
            return self
        return f


class Sch:
    ENG = ("pe", "act", "dve", "pool", "sp")

    def __init__(self):
        self.ops = {e: [] for e in self.ENG}
        self.track = {}
        self.known = {e: {} for e in self.ENG}
        self.dcnt = {}
        self.batch_end = {}
        self.signal = set()

    def _add(self, eng, fn, reads, writes, who, extra_deps=()):
        deps = set(extra_deps)
        rr = [_region(a) for a in reads]
        ww = [_region(a) for a in writes]
        for (nm, lo, hi, p0, p1) in rr:
            for e in self.track.get(nm, ()):
                if (e[2] or (nm == "P" and e[5][0] != who[0])) and e[0] < hi and lo < e[1] and e[3] < p1 and p0 < e[4]:
                    deps.add(e[5])
        for (nm, lo, hi, p0, p1) in ww:
            for e in self.track.get(nm, ()):
                if e[0] < hi and lo < e[1] and e[3] < p1 and p0 < e[4]:
                    deps.add(e[5])
        for (nm, lo, hi, p0, p1) in ww:
            lst = self.track.setdefault(nm, [])
            lst[:] = [e for e in lst if not (lo <= e[0] and e[1] <= hi and p0 <= e[3] and e[4] <= p1)]
            lst.append((lo, hi, True, p0, p1, who))
        for (nm, lo, hi, p0, p1) in rr:
            lst = self.track.setdefault(nm, [])
            lst[:] = [e for e in lst if not ((not e[2]) and e[5][0] == who[0] and lo <= e[0] and e[1] <= hi
                                             and p0 <= e[3] and e[4] <= p1)]
            lst.append((lo, hi, False, p0, p1, who))
        waits = []
        kn = self.known[eng]
        dmax = {}
        for (x, k) in deps:
            k = self.batch_end.get((x, k), k)
            if k > dmax.get(x, 0):
                dmax[x] = k
        for (x, k) in sorted(dmax.items()):
            if x == who[0] and x == "pe":
                continue
            if kn.get(x, 0) >= k:
                continue
            kn[x] = k
            waits.append((x, k))
            if not x.startswith("D:"):
                self.signal.add((x, k))
        rec = _Rec()
        fn(rec)
        self.ops[eng].append((rec.calls, waits, who))

    def op(self, eng, fn, reads=(), writes=()):
        who = (eng, len(self.ops[eng]) + 1)
        self._add(eng, fn, reads, writes, who)

    def dma(self, eng, out, in_, slot, serialize=True):
        n = self.dcnt.get(slot, 0) + 1
        self.dcnt[slot] = n
        who = ("D:" + slot, n)
        extra = [("D:" + slot, n - 1)] if (serialize and n > 1) else []
        self._add(eng, lambda e, o=out, i=in_: e.dma_start(out=o, in_=i), [in_], [out], who, extra)

    def emit(self, nc, block):
        sems = {}
        for e in ("pe", "act", "dve", "pool"):
            sems[e] = nc.alloc_semaphore(name="s_" + e)
        for s in self.dcnt:
            sems["D:" + s] = nc.alloc_semaphore(name="d_" + s)
        pref = {}
        for e in self.ENG:
            c = 0
            arr = [0]
            for i in range(1, len(self.ops[e]) + 1):
                if (e, i) in self.signal:
                    c += 1
                arr.append(c)
            pref[e] = arr

        def run(name, eh):
            for (fn, waits, who) in self.ops[name]:
                for (x, k) in waits:
                    v = 16 * k if x.startswith("D:") else pref[x][k]
                    eh.wait_ge(sems[x], v)
                ins = None
                for (mname, a, kw) in fn:
                    ins = getattr(eh, mname)(*a, **kw)
                if who[0].startswith("D:"):
                    ins.then_inc(sems[who[0]], 16)
                elif who in self.signal:
                    ins.then_inc(sems[name], 1)
            if name == "sp":
                for s, n in self.dcnt.items():
                    eh.wait_ge(sems["D:" + s], 16 * n)

        block.tensor(lambda e: run("pe", e))
        block.scalar(lambda e: run("act", e))
        block.vector(lambda e: run("dve", e))
        block.gpsimd(lambda e: run("pool", e))
        block.sync(lambda e: run("sp", e))


def build(cfg, mode, stop_after=4):
    c = cfg
    nc = bass.Bass("TRN2", target_bir_lowering=False)
    S = Sch()
    KD, H, DI, RB, G, R, FB, HH = c.KD, c.H, c.DI, c.RB, c.G, c.R, c.FB, c.HH

    def din(name, shape, dt=F32):
        return nc.dram_tensor(name, list(shape), dt, kind="ExternalInput").ap()

    xT = din("xT", [c.D, c.WIN + HALO3])
    vcol = din("vcol", [128, c.WIN // CH])
    vrow = din("vrow", [1, c.WIN + HALO3])
    kmask = din("kmask", [c.WIN // CH, 256])
    sel = din("sel", [128, c.NC])
    cpd = din("cp", [128, c.NCP])
    crd = din("cr", [1, c.NCR])
    cmd = din("cm", [128, 5 * 128 + 256])
    w_in = din("w_in", [c.D, c.IN])
    need_main = mode in ("main", "fused")
    if need_main:
        w_out = din("w_out", [DI, c.D])
        w_up = [din("w_up0", [c.D, 2 * c.DFF]), din("w_up1", [c.D, 2 * c.DFF])]
        w_dn = [din("w_dn0", [c.DFF, c.D]), din("w_dn1", [c.DFF, c.D])]
        w_kv = din("w_kv", [c.D, 2 * c.KVD])
        w_q = din("w_q", [c.D, c.D])
        w_o = din("w_o", [c.D, c.D])
        biasT = din("biasT", [c.KVH * 128, c.QPK * 256])
        yT = nc.dram_tensor("yT", [c.D, c.OWN * T], F32, kind="ExternalOutput").ap()
    SW = H * 64
    if mode == "p1":
        stF = nc.dram_tensor("stF", [128, SW], F32, kind="ExternalOutput").ap()
        stG = nc.dram_tensor("stG", [128, SW], F32, kind="ExternalOutput").ap()
        caF = nc.dram_tensor("caF", [128, H], F32, kind="ExternalOutput").ap()
        caG = nc.dram_tensor("caG", [128, H], F32, kind="ExternalOutput").ap()
    elif mode == "main":
        stFs = din("stFs", [c.NC * 128, SW])
        stGs = din("stGs", [c.NC * 128, SW])
        caFs = din("caFs", [c.NC * 128, H])
        caGs = din("caGs", [c.NC * 128, H])
    else:
        stF = nc.dram_tensor("stF", [128, SW], F32, kind="Internal").ap()
        stG = nc.dram_tensor("stG", [128, SW], F32, kind="Internal").ap()
        caF = nc.dram_tensor("caF", [128, H], F32, kind="Internal").ap()
        caG = nc.dram_tensor("caG", [128, H], F32, kind="Internal").ap()
        stFs = nc.dram_tensor("stFs", [c.NC * 128, SW], F32, kind="Internal").ap()
        stGs = nc.dram_tensor("stGs", [c.NC * 128, SW], F32, kind="Internal").ap()
        caFs = nc.dram_tensor("caFs", [c.NC * 128, H], F32, kind="Internal").ap()
        caGs = nc.dram_tensor("caGs", [c.NC * 128, H], F32, kind="Internal").ap()
    Sd = nc.dram_tensor("Sd", [128, SW], F32, kind="Internal").ap()

    def sb(name, shape, dt):
        return nc.alloc_sbuf_tensor(name, list(shape), dt)

    P = nc.alloc_psum_tensor("P", [128, 4096], F32)

    def bank(b, n=512, off=0):
        return P[:, b * 512 + off:b * 512 + off + n]

    def bank_bf(b, n, off=0):
        return P[:, b * 512:(b + 1) * 512].bitcast(BF16)[:, off:off + n]

    NW = 3
    import os as _os
    WK = int(_os.environ.get("WK", "16"))
    wbuf = [sb("wbuf%d" % i, [128, WK * 256], BF16) for i in range(NW)]
    hT = sb("hT", [128, KD, T], F32)
    hhalo = sb("hhalo", [128, KD, HALO3], F32)
    uT = sb("uT", [128, KD, TH], BF16)
    A_BYTES = max(DI // 128 * T * 2, FB * T * 2, 2 * KD * T * 2)
    arenaA = sb("arenaA", [128, A_BYTES // 2], BF16)
    B_BYTES = 44 * 1024
    arenaB = sb("arenaB", [128, B_BYTES // 2], BF16)
    Sg = [sb("Sg%d" % i, [128, R * 64], F32) for i in range(2)]
    cpt = sb("cpt", [128, c.NCP], F32)
    crt = sb("crt", [128, c.NCR], F32)
    cmt = sb("cmt", [128, 5 * 128 + 256], F32)
    cbf = sb("cbf", [128, 256], BF16)
    Abc = sb("Abc", [128, H], F32)
    vct = sb("vct", [128, c.WIN // CH], F32)
    selt = sb("selt", [128, c.NC], F32)
    vrt = sb("vrt", [128, TH], F32)
    rstd = sb("rstd", [128, TH], F32)
    rtmp = sb("rtmp", [128, TH], F32)
    sq = [sb("sq%d" % i, [128, TH], BF16) for i in range(2)]
    dtT = [sb("dtT%d" % i, [128, H], F32) for i in range(2)]
    dtA = [sb("dtA%d" % i, [128, H], F32) for i in range(2)]
    dec3 = [sb("dec3_%d" % i, [128, 3, H], F32) for i in range(2)]
    expA = [dec3[i][:, 0, :] for i in range(2)]
    wdec = [dec3[i][:, 1, :] for i in range(2)]
    decb = [dec3[i][:, 2, :] for i in range(2)]
    cumA = sb("cumA", [128, H], F32)
    fcar = sb("fcar", [128, 2 * FB, 2], F32)
    stat_t = sb("stat_t", [128, 8 * c.QPK], F32)
    KTt = sb("KTt", [128, c.KVH, 3 * CH], BF16)
    KTz = sb("KTz", [128, 2, 2 * CH], BF16)
    Vt = sb("Vt", [128, 3, c.KVD], BF16)

    Um = cmt[:, 0:128]
    Lst = cmt[:, 128:256]
    onesf = cmt[:, 256:384]
    identf = cmt[:, 384:512]
    amask = cmt[:, 640:896]
    onesb = cbf[:, 0:128]
    identb = cbf[:, 128:256]

    def cpv(name, i=None, n=1):
        o, ln = c.cp[name]
        if i is None:
            return cpt[:, o:o + ln]
        return cpt[:, o + i:o + i + n]

    def crv(name):
        o, ln = c.cr[name]
        return crt[:, o:o + ln]

    def carve(arena, off_bytes, shape, dt):
        es = mybir.dt.size(dt)
        n = 1
        for d in shape[1:]:
            n *= d
        v = arena[:, off_bytes // 2:(off_bytes + n * es) // 2]
        if dt != BF16:
            v = v.bitcast(dt)
        if len(shape) == 3:
            v = v.rearrange("p (a b) -> p a b", b=shape[2])
        elif len(shape) == 4:
            v = v.rearrange("p (a b c) -> p a b c", b=shape[2], c=shape[3])
        return v

    class Carver:
        def __init__(self, arena, nbytes):
            self.a, self.n, self.o = arena, nbytes, 0

        def get(self, shape, dt):
            es = mybir.dt.size(dt)
            n = es
            for d in shape[1:]:
                n *= d
            n = (n + 63) // 64 * 64
            v = carve(self.a, self.o, shape, dt)
            self.o += n
            assert self.o <= self.n, ("arena overflow", self.o, self.n)
            return v

    gslot = [0]

    def gdma(out, in_, eng="sp"):
        s = "g%d" % (gslot[0] % 8)
        gslot[0] += 1
        S.dma(eng, out, in_, s)

    gdma(cpt[:, :], cpd[:, :])
    gdma(crt[:, :], crd[0:1, :].partition_broadcast(128))
    gdma(cmt[:, :], cmd[:, :])
    gdma(vct[:, :], vcol[:, :])
    gdma(selt[:, :], sel[:, :])
    S.op("act", lambda e: e.activation(out=cbf[:, 0:256], in_=cmt[:, 256:512], func=AF.Copy),
         [cmt[:, 256:512]], [cbf[:, 0:256]])
    S.op("act", lambda e: e.activation(out=Abc[:, :], in_=crv("alog"), func=AF.Exp), [crv("alog")], [Abc[:, :]])
    S.op("dve", lambda e: e.tensor_scalar(out=Abc[:, :], in0=Abc[:, :], scalar1=-1.0, scalar2=None, op0=ALU.mult),
         [Abc[:, :]], [Abc[:, :]])

    wctr = [0]

    def load_wblock(W, k0, nk, segs):
        slot = wctr[0] % NW
        wctr[0] += 1
        wv = wbuf[slot][:, 0:nk * 256].rearrange("p (k c) -> p k c", c=256)
        n_before = S.dcnt.get("w%d" % slot, 0)
        for (dc, ncol, sc) in segs:
            for ka in range(0, nk, 8):
                kb = min(nk, ka + 8)
                src = W[(k0 + ka) * 128:(k0 + kb) * 128, dc:dc + ncol].rearrange("(k p) c -> p k c", p=128)
                S.dma("pool", wv[:, ka:kb, sc:sc + ncol], src, "w%d" % slot, serialize=False)
        n_after = S.dcnt["w%d" % slot]
        for kk in range(n_before + 1, n_after + 1):
            S.batch_end[("D:w%d" % slot, kk)] = n_after
        return wv

    accctr = [0]

    def gemm_a(W, nkt, groups, rhs_fn, ntok, evac, rhs_reads):
        for gi, blocks in enumerate(groups):
            accs = []
            for bi in range(len(blocks)):
                accs.append(bank(accctr[0] % 4, ntok))
                accctr[0] += 1
            if len(blocks) == 2 and blocks[1] == blocks[0] + 128:
                segs = [(blocks[0], 256, 0)]
            else:
                segs = [(b, 128, i * 128) for i, b in enumerate(blocks)]
            nparts = (nkt + WK - 1) // WK
            for pi in range(nparts):
                k0 = pi * WK
                nk = min(WK, nkt - k0)
                wv = load_wblock(W, k0, nk, segs)

                def mm(e, wv=wv, k0=k0, nk=nk, accs=accs, nb=len(blocks)):
                    ins = None
                    for bi in range(nb):
                        for k in range(nk):
                            ins = e.matmul(accs[bi], lhsT=wv[:, k, bi * 128:(bi + 1) * 128], rhs=rhs_fn(k0 + k),
                                           start=(k0 + k == 0), stop=(k0 + k == nkt - 1))
                    return ins
                S.op("pe", mm, [wv[:, :, :]] + rhs_reads, accs)
            for bi in range(len(blocks)):
                evac(gi, bi, accs[bi])

    def pairs(start, nblocks):
        out = []
        b = 0
        while b < nblocks:
            n = min(2, nblocks - b)
            out.append([start + (b + i) * 128 for i in range(n)])
            b += n
        return out

    def sumsq_rstd(srcs, n_feat, eps, width, use_valid):
        nb = len(srcs)
        pn = bank(7, width)
        for b, s_ap in enumerate(srcs):
            sqb = sq[b % 2][:, 0:width]
            S.op("act", lambda e, o=sqb, i=s_ap: e.activation(out=o, in_=i, func=AF.Square), [s_ap], [sqb])
            S.op("pe", lambda e, i=sqb, b=b: e.matmul(pn, lhsT=onesb, rhs=i, start=(b == 0), stop=(b == nb - 1)),
                 [sqb, onesb], [pn])
        S.op("dve", lambda e: e.tensor_scalar(out=rtmp[:, 0:width], in0=pn, scalar1=1.0 / n_feat, scalar2=eps,
                                              op0=ALU.mult, op1=ALU.add), [pn], [rtmp[:, 0:width]])
        S.op("act", lambda e: e.activation(out=rtmp[:, 0:width], in_=rtmp[:, 0:width], func=AF.Sqrt),
             [rtmp[:, 0:width]], [rtmp[:, 0:width]])
        S.op("dve", lambda e: e.reciprocal(out=rstd[:, 0:width], in_=rtmp[:, 0:width]),
             [rtmp[:, 0:width]], [rstd[:, 0:width]])
        if use_valid:
            vr = vrt[:, TH - width:TH]
            S.op("dve", lambda e: e.tensor_tensor(out=rstd[:, 0:width], in0=rstd[:, 0:width], in1=vr, op=ALU.mult),
                 [rstd[:, 0:width], vr], [rstd[:, 0:width]])

    def norm_h_to_u(gname, with_halo, use_valid):
        if with_halo:
            srcs_m = [hT[:, k, :] for k in range(KD)]
            srcs_h = [hhalo[:, k, :] for k in range(KD)]
            sumsq_rstd(srcs_h, c.D, 1e-6, HALO3, False)
            vr = vrt[:, 0:HALO3]
            S.op("dve", lambda e: e.tensor_tensor(out=rstd[:, 0:HALO3], in0=rstd[:, 0:HALO3], in1=vr, op=ALU.mult),
                 [rstd[:, 0:HALO3], vr], [rstd[:, 0:HALO3]])
            for k in range(KD):
                S.op("dve", lambda e, k=k: e.scalar_tensor_tensor(out=uT[:, k, 0:HALO3], in0=hhalo[:, k, :],
                                                                  scalar=cpv(gname, k), in1=rstd[:, 0:HALO3],
                                                                  op0=ALU.mult, op1=ALU.mult),
                     [hhalo[:, k, :], rstd[:, 0:HALO3], cpv(gname, k)], [uT[:, k, 0:HALO3]])
        srcs = [hT[:, k, :] for k in range(KD)]
        sumsq_rstd(srcs, c.D, 1e-6, T, use_valid)
        for k in range(KD):
            S.op("dve", lambda e, k=k: e.scalar_tensor_tensor(out=uT[:, k, HALO3:TH], in0=hT[:, k, :],
                                                              scalar=cpv(gname, k), in1=rstd[:, 0:T],
                                                              op0=ALU.mult, op1=ALU.mult),
                 [hT[:, k, :], rstd[:, 0:T], cpv(gname, k)], [uT[:, k, HALO3:TH]])

    def postnorm_add(srcT, gname):
        sumsq_rstd([srcT[:, k, :] for k in range(KD)], c.D, 1e-6, T, False)
        for k in range(KD):
            S.op("dve", lambda e, k=k: e.scalar_tensor_tensor(out=srcT[:, k, :], in0=srcT[:, k, :],
                                                              scalar=cpv(gname, k), in1=rstd[:, 0:T],
                                                              op0=ALU.mult, op1=ALU.mult),
                 [srcT[:, k, :], rstd[:, 0:T], cpv(gname, k)], [srcT[:, k, :]])
            S.op("dve", lambda e, k=k: e.tensor_tensor(out=hT[:, k, :], in0=hT[:, k, :], in1=srcT[:, k, :],
                                                       op=ALU.add),
                 [hT[:, k, :], srcT[:, k, :]], [hT[:, k, :]])

    def load_tile(ti):
        t0 = ti * T
        for k0 in range(0, KD, 8):
            k1 = min(KD, k0 + 8)
            gdma(hT[:, k0:k1, :], xT[k0 * 128:k1 * 128, t0 + HALO3:t0 + TH].rearrange("(k p) t -> p k t", p=128))
        for k0 in range(0, KD, 8):
            k1 = min(KD, k0 + 8)
            gdma(hhalo[:, k0:k1, :], xT[k0 * 128:k1 * 128, t0:t0 + HALO3].rearrange("(k p) t -> p k t", p=128))
        gdma(vrt[:, :], vrow[0:1, t0:t0 + TH].partition_broadcast(128))

    def dt_prep(ti):
        segs = [(2 * DI + 2 * c.GN, H, 0)]
        pd = [bank(5, H), bank(7, H)]
        nparts = (KD + WK - 1) // WK
        for pi in range(nparts):
            k0 = pi * WK
            nk = min(WK, KD - k0)
            wv = load_wblock(w_in, k0, nk, segs)

            def mm(e, wv=wv, k0=k0, nk=nk):
                ins = None
                for ch in range(2):
                    for k in range(nk):
                        ins = e.matmul(pd[ch], lhsT=uT[:, k0 + k, HALO3 + ch * CH:HALO3 + (ch + 1) * CH],
                                       rhs=wv[:, k, 0:H], start=(k0 + k == 0), stop=(k0 + k == KD - 1))
                return ins
            S.op("pe", mm, [wv[:, :, :], uT[:, :, :]], pd)
        for ch in range(2):
            d = dtT[ch][:, :]
            S.op("dve", lambda e, ch=ch, d=d: e.tensor_tensor(out=d, in0=pd[ch], in1=crv("dtb"), op=ALU.add),
                 [pd[ch], crv("dtb")], [d])
            S.op("act", lambda e, d=d: e.activation(out=d, in_=d, func=AF.Exp), [d], [d])
            S.op("act", lambda e, d=d: e.activation(out=d, in_=d, func=AF.Ln, bias=1.0), [d], [d])
            if DTL < 1:
                continue
            vc = vct[:, ti * 2 + ch:ti * 2 + ch + 1]
            S.op("dve", lambda e, d=d, vc=vc: e.tensor_scalar(out=d, in0=d, scalar1=vc, scalar2=None, op0=ALU.mult),
                 [d, vc], [d])
            a = dtA[ch][:, :]
            S.op("dve", lambda e, d=d, a=a: e.tensor_tensor(out=a, in0=d, in1=Abc[:, :], op=ALU.mult),
                 [d, Abc[:, :]], [a])
            if DTL < 2:
                continue
            trip = ((Um, expA[ch], 0), (Lst, wdec[ch], 128), (onesf, decb[ch], 256))
            pps = [bank(6, H, off) for (_, _, off) in trip]

            def mm3(e, a=a):
                ins = None
                for (mat, _, _), pp in zip(trip, pps):
                    ins = e.matmul(pp, lhsT=mat, rhs=a, start=True, stop=True)
                return ins
            S.op("pe", mm3, [Um, Lst, onesf, a], pps)
            if DTL < 3:
                continue
            if DTL < 4:
                continue
            p3 = bank(6, 384).rearrange("p (a b) -> p a b", b=128)[:, :, 0:H]
            S.op("act", lambda e, p3=p3, ch=ch: e.activation(out=dec3[ch][:, :, :], in_=p3, func=AF.Exp),
                 [p3], [dec3[ch][:, :, :]])
            S.op("dve", lambda e, pp=pps[2]: e.tensor_tensor(out=cumA[:, :], in0=cumA[:, :], in1=pp, op=ALU.add),
                 [cumA[:, :], pps[2], dec3[ch][:, :, :]], [cumA[:, :]])

    def conv_silu_evac(ps, blk, out_bf, out_f32=None):
        acc = convacc[blk % 2]
        o, _ = c.cp["scw"]
        w = lambda k: cpt[:, o + blk * 4 + k:o + blk * 4 + k + 1]
        b = cpv("scb", blk)
        S.op("act", lambda e: e.activation(out=acc, in_=ps[:, 0:T], func=AF.Identity, bias=b, scale=w(0)),
             [ps[:, 0:T], b, w(0)], [acc])
        for k in (1, 2, 3):
            S.op("dve", lambda e, k=k: e.scalar_tensor_tensor(out=acc, in0=ps[:, k:k + T], scalar=w(k), in1=acc,
                                                              op0=ALU.mult, op1=ALU.add),
                 [ps[:, k:k + T], w(k), acc], [acc])
        S.op("act", lambda e: e.activation(out=out_bf, in_=acc, func=AF.Silu), [acc], [out_bf])
        if out_f32 is not None:
            S.op("act", lambda e: e.activation(out=out_f32, in_=acc, func=AF.Silu), [acc], [out_f32])

    convacc = [None, None]

    def ssd_tile(ti, full):
        cb = Carver(arenaB, B_BYTES)
        convacc[0] = cb.get([128, T], F32)
        convacc[1] = cb.get([128, T], F32)
        xcT = cb.get([128, RB, T], BF16)
        BcT = cb.get([128, T], BF16)
        xdt = [cb.get([128, R * 64], BF16) for _ in range(2)]
        xdtw = cb.get([128, R * 64], BF16)
        Btok = [cb.get([128, 128], BF16) for _ in range(2)]
        if full:
            szT = cb.get([128, RB, T], BF16)
            CcT = cb.get([128, T], BF16)
            CcF = cb.get([128, T], F32)
            Rt = cb.get([128, HH, 128], F32)
            Ef = cb.get([128, HH, 128], F32)
            MT = cb.get([128, R, 128], BF16)
            cbm = cb.get([128, 128], F32)
            ytmp = cb.get([128, HH * 64], F32)
            ytok = cb.get([128, R * 64], BF16)
            t1 = cb.get([128, RB, CH], F32)
            ygT = cb.get([128, RB, T], BF16)
            ynT = carve(arenaA, 0, [128, DI // 128, T], BF16)
        dt_prep(ti)
        for g in range(min(G, int(_osx.environ.get('GLIM', '99')))):
            Sgt = Sg[g % 2]
            gdma(Sgt[:, :], Sd[:, g * R * 64:(g + 1) * R * 64])
            cols_x = pairs(DI + g * R * 64, RB)
            groups = list(cols_x) + [[2 * DI + g * 128]]
            kinds = [("x", i) for i in range(len(cols_x))] + [("B", 0)]
            if full:
                groups += [[2 * DI + c.GN + g * 128]] + pairs(g * R * 64, RB)
                kinds += [("C", 0)] + [("z", i) for i in range(len(cols_x))]

            def evac(gi, bi, ps, g=g, kinds=kinds):
                kind, i = kinds[gi]
                if kind == "x":
                    blk = i * 2 + bi
                    conv_silu_evac(ps, g * RB + blk, xcT[:, blk, :])
                elif kind == "B":
                    conv_silu_evac(ps, DI // 128 + g, BcT)
                elif kind == "C":
                    conv_silu_evac(ps, DI // 128 + G + g, CcT, CcF)
                else:
                    blk = i * 2 + bi
                    S.op("act", lambda e: e.activation(out=szT[:, blk, :], in_=ps[:, HALO3:TH], func=AF.Silu),
                         [ps[:, HALO3:TH]], [szT[:, blk, :]])
            gemm_a(w_in, KD, groups, lambda k: uT[:, k, :], TH, evac, [uT[:, :, :]])
            hs = slice(g * R, (g + 1) * R)
            for ch in range(2):
                tsl = slice(ch * CH, (ch + 1) * CH)
                pst = bank_bf(4, R * 64)
                def tr(e, ch=ch):
                    ins = None
                    for b in range(RB):
                        ins = e.transpose(out=pst[:, b * 128:(b + 1) * 128], in_=xcT[:, b, ch * CH:(ch + 1) * CH],
                                          identity=identb)
                    return ins
                S.op("pe", tr, [xcT[:, :, tsl], identb], [pst])
                dtb = dtT[ch][:, hs].unsqueeze(2).to_broadcast([128, R, 64])
                S.op("dve", lambda e, ch=ch, dtb=dtb: e.tensor_tensor(
                    out=xdt[ch].rearrange("p (r d) -> p r d", d=64), in0=pst.rearrange("p (r d) -> p r d", d=64),
                    in1=dtb, op=ALU.mult), [pst, dtT[ch][:, hs]], [xdt[ch]])
                pstB = bank_bf(5, 128, 256)
                S.op("pe", lambda e, ch=ch, pstB=pstB: e.transpose(out=pstB, in_=BcT[:, ch * CH:(ch + 1) * CH],
                                                                   identity=identb),
                     [BcT[:, tsl], identb], [pstB])
                S.op("act", lambda e, ch=ch, pstB=pstB: e.activation(out=Btok[ch], in_=pstB, func=AF.Copy),
                     [pstB], [Btok[ch]])
                if full and SSDL >= 1:
                    pcb = bank(5, 128, 384)
                    S.op("pe", lambda e, ch=ch, pcb=pcb: e.matmul(pcb, lhsT=BcT[:, ch * CH:(ch + 1) * CH],
                                                                  rhs=CcT[:, ch * CH:(ch + 1) * CH],
                                                                  start=True, stop=True),
                         [BcT[:, tsl], CcT[:, tsl]], [pcb])
                    S.op("dve", lambda e, pcb=pcb: e.tensor_tensor(out=cbm, in0=pcb, in1=Um, op=ALU.mult),
                         [pcb, Um], [cbm])
                    for hf in range(c.NHALF if SSDL >= 2 else 0):
                        h0 = g * R + hf * HH
                        da = dtA[ch][:, h0:h0 + HH]
                        S.op("dve", lambda e, da=da: e.tensor_tensor(
                            out=Rt, in0=da.unsqueeze(2).to_broadcast([128, HH, 128]),
                            in1=Um.unsqueeze(1).to_broadcast([128, HH, 128]), op=ALU.mult), [da, Um], [Rt])
                        for q0 in range(0, HH, 4):
                            qn = min(4, HH - q0)
                            pseg = bank(6, qn * 128)
                            S.op("pe", lambda e, q0=q0, qn=qn, pseg=pseg: e.matmul(
                                pseg, lhsT=Lst, rhs=Rt[:, q0:q0 + qn, :].rearrange("p a b -> p (a b)"),
                                start=True, stop=True), [Lst, Rt[:, q0:q0 + qn, :]], [pseg])
                            S.op("act", lambda e, q0=q0, qn=qn, pseg=pseg: e.activation(
                                out=Ef[:, q0:q0 + qn, :].rearrange("p a b -> p (a b)"), in_=pseg, func=AF.Exp),
                                [pseg], [Ef[:, q0:q0 + qn, :]])
                        mts = MT[:, hf * HH:(hf + 1) * HH, :]
                        S.op("dve", lambda e, mts=mts: e.tensor_tensor(
                            out=mts, in0=Ef, in1=cbm.unsqueeze(1).to_broadcast([128, HH, 128]), op=ALU.mult),
                            [Ef, cbm], [mts])
                    for hf in range(c.NHALF if SSDL >= 3 else 0):
                        cs = slice(hf * HH * 64, (hf + 1) * HH * 64)
                        py = bank(7, HH * 64)
                        def ydiag(e, ch=ch, hf=hf, py=py):
                            ins = None
                            for r in range(HH):
                                rr = hf * HH + r
                                ins = e.matmul(py[:, r * 64:(r + 1) * 64], lhsT=MT[:, rr, :],
                                               rhs=xdt[ch][:, rr * 64:(rr + 1) * 64], start=True, stop=True)
                            return ins
                        S.op("pe", ydiag, [MT[:, hf * HH:(hf + 1) * HH, :], xdt[ch][:, cs]], [py])
                        po = bank(5, HH * 64)
                        S.op("pe", lambda e, ch=ch, cs=cs, po=po: e.matmul(
                            po, lhsT=CcF[:, ch * CH:(ch + 1) * CH], rhs=Sgt[:, cs], start=True, stop=True),
                            [CcF[:, tsl], Sgt[:, cs]], [po])
                        eb = expA[ch][:, g * R + hf * HH:g * R + (hf + 1) * HH].unsqueeze(2).to_broadcast([128, HH, 64])
                        S.op("dve", lambda e, cs=cs, po=po, eb=eb: e.tensor_tensor(
                            out=ytmp[:, :].rearrange("p (r d) -> p r d", d=64),
                            in0=po.rearrange("p (r d) -> p r d", d=64), in1=eb, op=ALU.mult),
                            [po, expA[ch][:, hs]], [ytmp[:, :]])
                        S.op("dve", lambda e, cs=cs, py=py: e.tensor_tensor(out=ytok[:, cs], in0=ytmp[:, :], in1=py,
                                                                            op=ALU.add),
                             [ytmp[:, :], py], [ytok[:, cs]])
                wb = wdec[ch][:, hs].unsqueeze(2).to_broadcast([128, R, 64])
                S.op("dve", lambda e, ch=ch, wb=wb: e.tensor_tensor(
                    out=xdtw.rearrange("p (r d) -> p r d", d=64), in0=xdt[ch].rearrange("p (r d) -> p r d", d=64),
                    in1=wb, op=ALU.mult), [xdt[ch], wdec[ch][:, hs]], [xdtw])
                db = decb[ch][:, hs].unsqueeze(2).to_broadcast([128, R, 64])
                S.op("dve", lambda e, db=db: e.tensor_tensor(
                    out=Sgt[:, :].rearrange("p (r d) -> p r d", d=64),
                    in0=Sgt[:, :].rearrange("p (r d) -> p r d", d=64), in1=db, op=ALU.mult),
                    [Sgt[:, :], decb[ch][:, hs]], [Sgt[:, :]])
                for hf in range(c.NHALF):
                    cs = slice(hf * HH * 64, (hf + 1) * HH * 64)
                    pds = bank(6, HH * 64)
                    S.op("pe", lambda e, ch=ch, cs=cs, pds=pds: e.matmul(pds, lhsT=Btok[ch], rhs=xdtw[:, cs],
                                                                         start=True, stop=True),
                         [Btok[ch], xdtw[:, cs]], [pds])
                    S.op("dve", lambda e, cs=cs, pds=pds: e.tensor_tensor(out=Sgt[:, cs], in0=Sgt[:, cs], in1=pds,
                                                                          op=ALU.add),
                         [Sgt[:, cs], pds], [Sgt[:, cs]])
                if full and SSDL >= 4:
                    pyt = bank_bf(4, RB * 128)
                    def tr2(e):
                        ins = None
                        for b in range(RB):
                            ins = e.transpose(out=pyt[:, b * 128:(b + 1) * 128], in_=ytok[:, b * 128:(b + 1) * 128],
                                              identity=identb)
                        return ins
                    S.op("pe", tr2, [ytok, identb], [pyt])
                    dcb = cpv("dc")[:, g * RB:(g + 1) * RB].unsqueeze(2).to_broadcast([128, RB, CH])
                    S.op("dve", lambda e, tsl=tsl, dcb=dcb: e.tensor_tensor(out=t1[:, :, :], in0=xcT[:, :, tsl],
                                                                            in1=dcb, op=ALU.mult),
                         [xcT[:, :, tsl], cpv("dc")], [t1[:, :, :]])
                    S.op("dve", lambda e, tsl=tsl, pyt=pyt: e.tensor_tensor(
                        out=t1[:, :, :], in0=t1[:, :, :], in1=pyt.rearrange("p (b t) -> p b t", t=CH), op=ALU.add),
                        [t1[:, :, :], pyt], [t1[:, :, :]])
                    S.op("dve", lambda e, tsl=tsl: e.tensor_tensor(out=ygT[:, :, tsl], in0=t1[:, :, :],
                                                                   in1=szT[:, :, tsl], op=ALU.mult),
                         [t1[:, :, :], szT[:, :, tsl]], [ygT[:, :, tsl]])
            gdma(Sd[:, g * R * 64:(g + 1) * R * 64], Sgt[:, :])
            if full and SSDL >= 5:
                sumsq_rstd([ygT[:, b, :] for b in range(RB)], R * 64, 1e-5, T, False)
                for b in range(RB):
                    S.op("dve", lambda e, b=b: e.scalar_tensor_tensor(
                        out=ynT[:, g * RB + b, :], in0=ygT[:, b, :], scalar=cpv("gn", g * RB + b), in1=rstd[:, 0:T],
                        op0=ALU.mult, op1=ALU.mult),
                        [ygT[:, b, :], cpv("gn", g * RB + b), rstd[:, 0:T]], [ynT[:, g * RB + b, :]])
        if full:
            mixT = carve(arenaB, 0, [128, KD, T], F32)

            def evo(gi, bi, ps):
                blk = gi * 2 + bi
                S.op("act", lambda e: e.activation(out=mixT[:, blk, :], in_=ps, func=AF.Copy), [ps], [mixT[:, blk, :]])
            import os as _os3
            if _os3.environ.get("SKIPO"):
                return
            gemm_a(w_out, DI // 128, pairs(0, KD), lambda k: ynT[:, k, :], T, evo, [ynT[:, :, :]])
            postnorm_add(mixT, "nmq0")

    def ffn(layer):
        norm_h_to_u("nfp%d" % layer, False, True)
        cb = Carver(arenaB, B_BYTES)
        st = [cb.get([128, T + 2], F32) for _ in range(2)]
        acc = [cb.get([128, T], F32) for _ in range(2)]
        sg = cb.get([128, 2, T], BF16)
        actT = carve(arenaA, 0, [128, FB, T], BF16)
        ow, _ = c.cp["fcw%d" % layer]
        ob, _ = c.cp["fcb%d" % layer]
        groups, kinds = [], []
        for pr in pairs(0, FB):
            j0 = pr[0] // 128
            groups.append(pr)
            kinds.append(("g", j0))
            groups.append([c.DFF + p for p in pr])
            kinds.append(("v", j0))
        ectr = [0]

        def evac(gi, bi, ps):
            kind, j0 = kinds[gi]
            j = j0 + bi
            blk = j if kind == "g" else FB + j
            i = ectr[0] % 2
            ectr[0] += 1
            s_, a_ = st[i], acc[i]
            car = fcar[:, blk, :]
            w = lambda k: cpt[:, ow + blk * 3 + k:ow + blk * 3 + k + 1]
            b = cpt[:, ob + blk:ob + blk + 1]
            S.op("act", lambda e: e.activation(out=s_[:, 2:T + 2], in_=ps, func=AF.Copy), [ps], [s_[:, 2:T + 2]])
            S.op("dve", lambda e: e.tensor_copy(out=s_[:, 0:2], in_=car), [car], [s_[:, 0:2]])
            S.op("dve", lambda e: e.tensor_copy(out=car, in_=s_[:, T:T + 2]), [s_[:, T:T + 2]], [car])
            S.op("act", lambda e: e.activation(out=a_, in_=s_[:, 0:T], func=AF.Identity, bias=b, scale=w(0)),
                 [s_[:, 0:T], b, w(0)], [a_])
            for k in (1, 2):
                S.op("dve", lambda e, k=k: e.scalar_tensor_tensor(out=a_, in0=s_[:, k:k + T], scalar=w(k), in1=a_,
                                                                  op0=ALU.mult, op1=ALU.add),
                     [s_[:, k:k + T], w(k), a_], [a_])
            if kind == "g":
                S.op("act", lambda e: e.activation(out=sg[:, bi, :], in_=a_, func=AF.Silu), [a_], [sg[:, bi, :]])
            else:
                S.op("dve", lambda e: e.tensor_tensor(out=actT[:, j, :], in0=a_, in1=sg[:, bi, :], op=ALU.mult),
                     [a_, sg[:, bi, :]], [actT[:, j, :]])
        gemm_a(w_up[layer], KD, groups, lambda k: uT[:, k, HALO3:TH], T, evac, [uT[:, :, :]])
        fT = carve(arenaB, 0, [128, KD, T], F32)

        def evo(gi, bi, ps):
            blk = gi * 2 + bi
            S.op("act", lambda e: e.activation(out=fT[:, blk, :], in_=ps, func=AF.Copy), [ps], [fT[:, blk, :]])
        gemm_a(w_dn[layer], FB, pairs(0, KD), lambda k: actT[:, k, :], T, evo, [actT[:, :, :]])
        postnorm_add(fT, "nfq%d" % layer)

    def attn(ti):
        KVH, QPK = c.KVH, c.QPK
        cb = Carver(arenaB, B_BYTES)
        maskb = cb.get([128, QPK, 256], F32)
        kmt = cb.get([128, 2, 256], F32)
        sc = cb.get([128, QPK, 256], F32)
        pp = cb.get([128, QPK, 256], BF16)
        pT = cb.get([128, 2, QPK * 128], BF16)
        stat = stat_t[:, :]
        QT = carve(arenaA, 0, [128, KD, T], BF16)
        otok = carve(arenaA, KD * T * 2, [128, 2, c.D], BF16)
        mx, negm, rsum, sterm, den, rden = [stat[:, i * QPK:(i + 1) * QPK] for i in range(6)]
        S.op("act", lambda e: e.activation(out=KTt[:, :, 0:CH], in_=KTt[:, :, 2 * CH:3 * CH], func=AF.Copy),
             [KTt[:, :, 2 * CH:3 * CH]], [KTt[:, :, 0:CH]])
        S.op("act", lambda e: e.activation(out=Vt[:, 0, :], in_=Vt[:, 2, :], func=AF.Copy), [Vt[:, 2, :]], [Vt[:, 0, :]])
        norm_h_to_u("kvn", False, False)
        for g in range(KVH):
            segs = [(g * 64, 64, 0), (g * 64, 64, 64)]
            acc = bank(accctr[0] % 4, T)
            accctr[0] += 1
            nparts = (KD + WK - 1) // WK
            for pi in range(nparts):
                k0 = pi * WK
                nk = min(WK, KD - k0)
                wv = load_wblock(w_kv, k0, nk, segs)

                def mm(e, wv=wv, k0=k0, nk=nk, acc=acc):
                    ins = None
                    for k in range(nk):
                        ins = e.matmul(acc, lhsT=wv[:, k, 0:128], rhs=uT[:, k0 + k, HALO3:TH],
                                       start=(k0 + k == 0), stop=(k0 + k == KD - 1))
                    return ins
                S.op("pe", mm, [wv[:, :, :], uT[:, :, :]], [acc])
            S.op("act", lambda e, g=g, acc=acc: e.activation(out=KTt[:, g, CH:3 * CH], in_=acc, func=AF.Identity,
                                                             bias=cpv("bk", g)),
                 [acc, cpv("bk", g)], [KTt[:, g, CH:3 * CH]])
        for v0 in range(0, c.KVD, 256):
            vn = min(256, c.KVD - v0)
            segs = [(c.KVD + v0, vn, 0)]
            pv = [bank(accctr[0] % 4, vn), bank((accctr[0] + 1) % 4, vn)]
            accctr[0] += 2
            nparts = (KD + WK - 1) // WK
            for pi in range(nparts):
                k0 = pi * WK
                nk = min(WK, KD - k0)
                wv = load_wblock(w_kv, k0, nk, segs)

                def mm(e, wv=wv, k0=k0, nk=nk, pv=pv, vn=vn):
                    ins = None
                    for ch in range(2):
                        for k in range(nk):
                            ins = e.matmul(pv[ch], lhsT=uT[:, k0 + k, HALO3 + ch * CH:HALO3 + (ch + 1) * CH],
                                           rhs=wv[:, k, 0:vn], start=(k0 + k == 0), stop=(k0 + k == KD - 1))
                    return ins
                S.op("pe", mm, [wv[:, :, :], uT[:, :, :]], pv)
            for ch in range(2):
                bvs = crv("bv")[:, v0:v0 + vn]
                S.op("dve", lambda e, ch=ch, v0=v0, vn=vn, bvs=bvs, pv=pv: e.tensor_tensor(
                    out=Vt[:, 1 + ch, v0:v0 + vn], in0=pv[ch], in1=bvs, op=ALU.add),
                    [pv[ch], bvs], [Vt[:, 1 + ch, v0:v0 + vn]])
        if DBG < 1:
            return
        norm_h_to_u("nmp1", False, False)

        def evq(gi, bi, ps):
            blk = gi * 2 + bi
            S.op("act", lambda e: e.activation(out=QT[:, blk, :], in_=ps, func=AF.Identity, bias=cpv("bq", blk)),
                 [ps, cpv("bq", blk)], [QT[:, blk, :]])
        gemm_a(w_q, KD, pairs(0, KD), lambda k: uT[:, k, HALO3:TH], T, evq, [uT[:, :, :]])
        if DBG < 1.1:
            return
        for qb in range(2):
            gdma(kmt[:, qb, :], kmask[ti * 2 + qb:ti * 2 + qb + 1, :].partition_broadcast(128))
        for g in range(KVH):
            for qb in range(2):
                gdma(maskb[:, :, :], biasT[g * 128:(g + 1) * 128, :].rearrange("p (r k) -> p r k", k=256))
                S.op("dve", lambda e: e.tensor_tensor(out=maskb[:, :, :], in0=maskb[:, :, :],
                                                      in1=amask.unsqueeze(1).to_broadcast([128, QPK, 256]), op=ALU.add),
                     [maskb[:, :, :], amask], [maskb[:, :, :]])
                S.op("dve", lambda e, qb=qb: e.tensor_tensor(out=maskb[:, :, :], in0=maskb[:, :, :],
                                                             in1=kmt[:, qb, :].unsqueeze(1).to_broadcast([128, QPK, 256]),
                                                             op=ALU.add),
                     [maskb[:, :, :], kmt[:, qb, :]], [maskb[:, :, :]])
                if DBG < 1.2:
                    continue
                keys = slice(qb * CH, qb * CH + 2 * CH)
                for v in range(2):
                    ps_ = slice(v * 64, (v + 1) * 64)
                    S.op("act", lambda e, g=g, v=v, ps_=ps_, keys=keys: e.activation(
                        out=KTz[ps_, v, :], in_=KTt[ps_, g, keys], func=AF.Copy),
                        [KTt[ps_, g, keys]], [KTz[ps_, v, :]])
                for r0 in range(0, QPK, 2):
                    rn = min(2, QPK - r0)
                    ps = bank(4 + (r0 // 2) % 2, rn * 256)

                    def qk(e, g=g, qb=qb, r0=r0, rn=rn, ps=ps):
                        ins = None
                        for r in range(r0, r0 + rn):
                            hq = g * QPK + r
                            hb, hf = hq // 2, hq % 2
                            ins = e.matmul(ps[:, (r - r0) * 256:(r - r0 + 1) * 256],
                                           lhsT=QT[:, hb, qb * CH:(qb + 1) * CH],
                                           rhs=KTz[:, hf, :],
                                           start=True, stop=True)
                        return ins
                    S.op("pe", qk, [QT[:, :, qb * CH:(qb + 1) * CH], KTz[:, :, :]], [ps])
                    if DBG < 1.3:
                        continue
                    S.op("dve", lambda e, r0=r0, rn=rn, ps=ps: e.scalar_tensor_tensor(
                        out=sc[:, r0:r0 + rn, :].rearrange("p a b -> p (a b)"), in0=ps, scalar=0.125,
                        in1=maskb[:, r0:r0 + rn, :].rearrange("p a b -> p (a b)"), op0=ALU.mult, op1=ALU.add),
                        [ps, maskb[:, r0:r0 + rn, :]], [sc[:, r0:r0 + rn, :]])
                if DBG < 1.5:
                    continue
                S.op("dve", lambda e: e.tensor_reduce(out=mx, in_=sc[:, :, :], axis=AX.X, op=ALU.max),
                     [sc[:, :, :]], [mx])
                sk = crv("sink")[:, g * QPK:(g + 1) * QPK]
                S.op("dve", lambda e, sk=sk: e.tensor_tensor(out=mx, in0=mx, in1=sk, op=ALU.max), [mx, sk], [mx])
                S.op("dve", lambda e: e.tensor_scalar(out=negm, in0=mx, scalar1=-1.0, scalar2=None, op0=ALU.mult),
                     [mx], [negm])
                S.op("dve", lambda e, sk=sk: e.tensor_tensor(out=sterm, in0=sk, in1=mx, op=ALU.subtract),
                     [sk, mx], [sterm])
                S.op("act", lambda e: e.activation(out=sterm, in_=sterm, func=AF.Exp), [sterm], [sterm])
                if DBG < 1.8:
                    continue
                for r in range(QPK):
                    S.op("act", lambda e, r=r: e.activation(out=pp[:, r, :], in_=sc[:, r, :], func=AF.Exp,
                                                            bias=negm[:, r:r + 1]),
                         [sc[:, r, :], negm[:, r:r + 1]], [pp[:, r, :]])
                if DBG < 1.9:
                    continue
                S.op("dve", lambda e: e.tensor_reduce(out=rsum, in_=pp[:, :, :], axis=AX.X, op=ALU.add),
                     [pp[:, :, :]], [rsum])
                if DBG < 1.95:
                    continue
                S.op("dve", lambda e: e.tensor_tensor(out=den, in0=rsum, in1=sterm, op=ALU.add), [rsum, sterm], [den])
                if DBG < 1.98:
                    continue
                if DBG == 2.5:
                    S.op("dve", lambda e: e.tensor_copy(out=rden, in_=den), [den], [rden])
                    continue
                S.op("dve", lambda e: e.reciprocal(out=rden, in_=den), [den], [rden])
                if DBG < 3:
                    continue
                for kb in range(2):
                    ptp = bank_bf(6, QPK * 128)
                    def trp(e, kb=kb, ptp=ptp):
                        ins = None
                        for r in range(QPK):
                            ins = e.transpose(out=ptp[:, r * 128:(r + 1) * 128], in_=pp[:, r, kb * CH:(kb + 1) * CH],
                                              identity=identb)
                        return ins
                    S.op("pe", trp, [pp[:, :, kb * CH:(kb + 1) * CH], identb], [ptp])
                    S.op("act", lambda e, kb=kb, ptp=ptp: e.activation(out=pT[:, kb, :], in_=ptp, func=AF.Copy),
                         [ptp], [pT[:, kb, :]])
                if DBG < 4:
                    continue
                po = bank(7, QPK * 64)

                def pv_(e, g=g, qb=qb, po=po):
                    ins = None
                    for r in range(QPK):
                        for kb in range(2):
                            ins = e.matmul(po[:, r * 64:(r + 1) * 64], lhsT=pT[:, kb, r * 128:(r + 1) * 128],
                                           rhs=Vt[:, qb + kb, g * 64:(g + 1) * 64], start=(kb == 0), stop=(kb == 1))
                    return ins
                S.op("pe", pv_, [pT[:, :, :], Vt[:, qb:qb + 2, g * 64:(g + 1) * 64]], [po])
                od = otok[:, qb, g * QPK * 64:(g + 1) * QPK * 64]
                S.op("dve", lambda e, po=po, od=od: e.tensor_tensor(
                    out=od.rearrange("p (r d) -> p r d", d=64), in0=po.rearrange("p (r d) -> p r d", d=64),
                    in1=rden.unsqueeze(2).to_broadcast([128, QPK, 64]), op=ALU.mult), [po, rden], [od])
        if DBG < 5:
            return
        for qb in range(2):
            for b0 in range(0, KD, 8):
                bn = min(8, KD - b0)
                pt = bank_bf(4 + (b0 // 8) % 2, bn * 128)

                def tro(e, qb=qb, b0=b0, bn=bn, pt=pt):
                    ins = None
                    for b in range(bn):
                        ins = e.transpose(out=pt[:, b * 128:(b + 1) * 128],
                                          in_=otok[:, qb, (b0 + b) * 128:(b0 + b + 1) * 128], identity=identb)
                    return ins
                S.op("pe", tro, [otok[:, qb, b0 * 128:(b0 + bn) * 128], identb], [pt])
                dst = uT[:, b0:b0 + bn, HALO3 + qb * CH:HALO3 + (qb + 1) * CH]
                S.op("act", lambda e, pt=pt, dst=dst: e.activation(out=dst, in_=pt.rearrange("p (b t) -> p b t", t=CH),
                                                                   func=AF.Copy), [pt], [dst])
        aT = carve(arenaB, 0, [128, KD, T], F32)

        def evo(gi, bi, ps):
            blk = gi * 2 + bi
            S.op("act", lambda e: e.activation(out=aT[:, blk, :], in_=ps, func=AF.Identity, bias=cpv("bo", blk)),
                 [ps, cpv("bo", blk)], [aT[:, blk, :]])
        gemm_a(w_o, KD, pairs(0, KD), lambda k: uT[:, k, HALO3:TH], T, evo, [uT[:, :, :]])
        postnorm_add(aT, "nmq1")

    zt = carve(arenaB, 0, [128, R * 64], F32)

    def zero_state():
        S.op("dve", lambda e: e.memset(zt, 0.0), [], [zt])
        for g in range(G):
            gdma(Sd[:, g * R * 64:(g + 1) * R * 64], zt)

    S.op("dve", lambda e: e.memset(cumA[:, :], 0.0), [], [cumA[:, :]])
    if mode in ("p1", "fused"):
        zero_state()
        P1S = int(_osx.environ.get('P1S', '9'))
        for ti in range(1, c.NT):
            if P1S < 1:
                continue
            load_tile(ti)
            if P1S < 2:
                continue
            norm_h_to_u("nmp0", True, True)
            if P1S < 3:
                continue
            ssd_tile(ti, False)
            if P1S < 4:
                continue
            if ti >= c.NT - 2:
                dst_s, dst_c = (stG, caG) if ti == c.NT - 2 else (stF, caF)
                if c.NT - 2 == 0 and ti == c.NT - 1:
                    pass
                for g in range(G):
                    tb = Sg[g % 2]
                    gdma(tb[:, :], Sd[:, g * R * 64:(g + 1) * R * 64])
                    gdma(dst_s[:, g * R * 64:(g + 1) * R * 64], tb[:, :])
                gdma(dst_c[:, :], cumA[:, :])
    if mode == "fused":
        rg = [list(range(c.NC))]
        for (src, dst) in ((stF, stFs), (stG, stGs), (caF, caFs), (caG, caGs)):
            n = S.dcnt.get("cc", 0) + 1
            S.dcnt["cc"] = n
            S._add("pool", lambda e, s=src, d=dst: e.collective_compute(
                "AllGather", ALU.bypass, replica_groups=rg, ins=[s[:, :]], outs=[d[:, :]]),
                [src[:, :]], [dst[:, :]], ("D:cc", n), [("D:cc", n - 1)] if n > 1 else [])
    if need_main:
        cb = Carver(arenaB, B_BYTES)
        W = R * 64
        Tt = cb.get([128, W], F32)
        Wa = cb.get([128, W], F32)
        tmp = cb.get([128, W], F32)
        Gl = [cb.get([128, W], F32) for _ in range(2)]
        Fl = [cb.get([128, W], F32) for _ in range(2)]
        eG = carve(arenaA, 0, [128, c.NC, H], F32)
        eF = carve(arenaA, c.NC * H * 4, [128, c.NC, H], F32)
        for cc_ in range(c.NC - 1):
            gdma(eG[:, cc_, :], caGs[cc_ * 128:(cc_ + 1) * 128, :])
            gdma(eF[:, cc_, :], caFs[cc_ * 128:(cc_ + 1) * 128, :])
            S.op("act", lambda e, cc_=cc_: e.activation(out=eG[:, cc_, :], in_=eG[:, cc_, :], func=AF.Exp),
                 [eG[:, cc_, :]], [eG[:, cc_, :]])
            S.op("act", lambda e, cc_=cc_: e.activation(out=eF[:, cc_, :], in_=eF[:, cc_, :], func=AF.Exp),
                 [eF[:, cc_, :]], [eF[:, cc_, :]])
        r3 = lambda a: a.rearrange("p (r d) -> p r d", d=64)
        import os as _os2
        for g in range(G if not _os2.environ.get("SKIPC") else 0):
            S.op("dve", lambda e: e.memset(Tt, 0.0), [], [Tt])
            S.op("dve", lambda e: e.memset(Wa, 0.0), [], [Wa])
            for cc_ in range(c.NC - 1):
                gl, fl = Gl[cc_ % 2], Fl[cc_ % 2]
                gdma(gl, stGs[cc_ * 128:(cc_ + 1) * 128, g * W:(g + 1) * W])
                gdma(fl, stFs[cc_ * 128:(cc_ + 1) * 128, g * W:(g + 1) * W])
                egb = eG[:, cc_, g * R:(g + 1) * R].unsqueeze(2).to_broadcast([128, R, 64])
                efb = eF[:, cc_, g * R:(g + 1) * R].unsqueeze(2).to_broadcast([128, R, 64])
                S.op("dve", lambda e, egb=egb: e.tensor_tensor(out=r3(tmp), in0=r3(Tt), in1=egb, op=ALU.mult),
                     [Tt, eG[:, cc_, :]], [tmp])
                S.op("dve", lambda e, gl=gl: e.tensor_tensor(out=tmp, in0=tmp, in1=gl, op=ALU.add), [tmp, gl], [tmp])
                sc_ = selt[:, cc_:cc_ + 1]
                S.op("dve", lambda e, sc_=sc_: e.scalar_tensor_tensor(out=Wa, in0=tmp, scalar=sc_, in1=Wa,
                                                                      op0=ALU.mult, op1=ALU.add),
                     [tmp, sc_, Wa], [Wa])
                S.op("dve", lambda e, efb=efb: e.tensor_tensor(out=r3(Tt), in0=r3(Tt), in1=efb, op=ALU.mult),
                     [Tt, eF[:, cc_, :]], [Tt])
                S.op("dve", lambda e, fl=fl: e.tensor_tensor(out=Tt, in0=Tt, in1=fl, op=ALU.add), [Tt, fl], [Tt])
            gdma(Sd[:, g * W:(g + 1) * W], Wa)
        S.op("dve", lambda e: e.memset(fcar[:, :, :], 0.0), [], [fcar[:, :, :]])
        S.op("dve", lambda e: e.memset(KTt[:, :, :], 0.0), [], [KTt[:, :, :]])
        S.op("dve", lambda e: e.memset(KTz[:, :, :], 0.0), [], [KTz[:, :, :]])
        S.op("dve", lambda e: e.memset(Vt[:, :, :], 0.0), [], [Vt[:, :, :]])
        fcar1 = sb("fcar1", [128, 2 * FB, 2], F32)
        S.op("dve", lambda e: e.memset(fcar1[:, :, :], 0.0), [], [fcar1[:, :, :]])
        fc0 = fcar
        for ti in range(c.NT):
            load_tile(ti)
            norm_h_to_u("nmp0", True, True)
            ssd_tile(ti, True)
            if stop_after >= 2:
                fcar = fc0
                ffn(0)
            if stop_after >= 3:
                attn(ti)
            if stop_after >= 4:
                fcar = fcar1
                ffn(1)
                fcar = fc0
            if ti >= 1:
                for k0 in range(0, KD, 8):
                    k1 = min(KD, k0 + 8)
                    gdma(yT[k0 * 128:k1 * 128, (ti - 1) * T:ti * T].rearrange("(k p) t -> p k t", p=128),
                         hT[:, k0:k1, :])
    with nc.Block() as block:
        S.emit(nc, block)
    return nc


def _t5_bucket_np(rel):
    import math
    max_exact = 16
    relf = np.maximum(rel, 1).astype(np.float32)
    large = max_exact + (np.log(relf / max_exact) / math.log(128 / max_exact) * (32 - max_exact)).astype(np.int32)
    large = np.minimum(large, 31)
    return np.where(rel < max_exact, rel, large)


def _pp(v):
    v = np.asarray(v, np.float32)
    return np.ascontiguousarray(v.reshape(-1, 128).T)


def prep_inputs(cfg, inp):
    c = cfg
    f = lambda a: np.ascontiguousarray(np.asarray(a, np.float32))
    x = f(inp["x"])[0]
    cp = np.zeros((128, c.NCP), np.float32)

    def put(name, arr):
        o, n = c.cp[name]
        assert arr.shape == (128, n), (name, arr.shape, n)
        cp[:, o:o + n] = arr
    for i in range(2):
        put("nmp%d" % i, _pp(inp["norm_mix_pre"][i]))
        put("nmq%d" % i, _pp(inp["norm_mix_post"][i]))
        put("nfp%d" % i, _pp(inp["norm_ffn_pre"][i]))
        put("nfq%d" % i, _pp(inp["norm_ffn_post"][i]))
        fw = f(inp["ffn_conv_w"][i])
        put("fcw%d" % i, np.ascontiguousarray(fw.T.reshape(2 * c.FB, 128, 3).transpose(1, 0, 2).reshape(128, -1)))
        put("fcb%d" % i, _pp(inp["ffn_conv_b"][i]))
    put("kvn", _pp(inp["kv_norm"]))
    sw = f(inp["ssm_conv_w"][0])
    put("scw", np.ascontiguousarray(sw.T.reshape(c.CB, 128, 4).transpose(1, 0, 2).reshape(128, -1)))
    put("scb", _pp(inp["ssm_conv_b"][0]))
    put("gn", _pp(inp["ssm_norm"][0]))
    put("dc", _pp(np.repeat(f(inp["ssm_d"][0]), 64)))
    put("bq", _pp(inp["attn_b_q"][0]))
    put("bo", _pp(inp["attn_b_o"][0]))
    bk = f(inp["b_kv"])[:c.KVD].reshape(c.KVH, 64)
    put("bk", np.ascontiguousarray(np.concatenate([bk, bk], axis=1).T))
    cr = np.zeros((1, c.NCR), np.float32)
    for name, arr in (("dtb", inp["ssm_dt_bias"][0]), ("alog", inp["ssm_a_log"][0]),
                      ("bv", f(inp["b_kv"])[c.KVD:]), ("sink", inp["attn_sinks"][0])):
        o, n = c.cr[name]
        cr[0, o:o + n] = f(arr)
    j = np.arange(128)
    cm = np.zeros((128, 5 * 128 + 256), np.float32)
    cm[:, 0:128] = (j[:, None] <= j[None, :])
    cm[:, 128:256] = (j[:, None] > j[None, :])
    cm[:, 256:384] = 1.0
    cm[:, 384:512] = np.eye(128)
    k = np.arange(256)
    rel = j[:, None] - k[None, :] + 128
    cm[:, 640:896] = np.where((rel >= 0) & (rel < 128), 0.0, NEG)
    rb = f(inp["rel_bias"])
    bidx = _t5_bucket_np(np.maximum(rel, 0))
    bias = rb[bidx]
    biasT = np.ascontiguousarray(bias.transpose(2, 0, 1).reshape(c.KVH, c.QPK, 128, 256).transpose(0, 2, 1, 3)
                                 .reshape(c.KVH * 128, c.QPK * 256))
    shared = dict(cp=cp, cr=cr, cm=cm, biasT=biasT,
                  w_in=f(inp["ssm_w_in"][0]), w_out=f(inp["ssm_w_out"][0]),
                  w_up0=f(inp["ffn_w_up"][0]), w_up1=f(inp["ffn_w_up"][1]),
                  w_dn0=f(inp["ffn_w_down"][0]), w_dn1=f(inp["ffn_w_down"][1]),
                  w_kv=f(inp["w_kv"]), w_q=f(inp["attn_w_q"][0]), w_o=f(inp["attn_w_o"][0]))
    per_core = []
    own = c.OWN * T
    for ci in range(c.NC):
        s = ci * own
        lo = s - T - HALO3
        xw = np.zeros((c.WIN + HALO3, c.D), np.float32)
        a = max(lo, 0)
        xw[a - lo:, :] = x[a:s + own]
        tok = np.arange(lo, s + own)
        valid = (tok >= 0).astype(np.float32)
        vrow = valid[None, :].copy()
        vcol = np.ascontiguousarray(valid[HALO3:].reshape(-1, 128).T)
        nblk = c.WIN // CH
        km = np.zeros((nblk, 256), np.float32)
        for b in range(nblk):
            kp = s - T + (b - 1) * CH + np.arange(256)
            km[b] = np.where(kp >= 0, 0.0, NEG)
        sel = np.zeros((128, c.NC), np.float32)
        if ci >= 1:
            sel[:, ci - 1] = 1.0
        per_core.append(dict(xT=np.ascontiguousarray(xw.T), vrow=vrow, vcol=vcol, kmask=km, sel=sel))
    return shared, per_core


_P1_KEYS = ("xT", "vcol", "vrow", "kmask", "sel", "cp", "cr", "cm", "w_in")
_NC_CACHE = {}


def run(cfg, inp, fused=True, stop_after=4):
    c = cfg
    shared, per_core = prep_inputs(c, inp)
    cores = list(range(c.NC))
    if fused:
        key = ("fused", stop_after, c.D, c.OWN)
        if key not in _NC_CACHE:
            _NC_CACHE[key] = build(c, "fused", stop_after)
        maps = [dict(shared, **pc) for pc in per_core]
        res = run_bass_kernel_spmd(_NC_CACHE[key], maps, core_ids=cores)
    else:
        key = ("p1", c.D, c.OWN)
        if key not in _NC_CACHE:
            _NC_CACHE[key] = build(c, "p1")
        maps1 = [{k: dict(shared, **pc)[k] for k in _P1_KEYS} for pc in per_core]
        r1 = run_bass_kernel_spmd(_NC_CACHE[key], maps1, core_ids=cores)
        gath = {n + "s": np.ascontiguousarray(np.concatenate([r[n] for r in r1.results], axis=0))
                for n in ("stF", "stG", "caF", "caG")}
        key = ("main", stop_after, c.D, c.OWN)
        if key not in _NC_CACHE:
            _NC_CACHE[key] = build(c, "main", stop_after)
        maps = [dict(shared, **pc, **gath) for pc in per_core]
        res = run_bass_kernel_spmd(_NC_CACHE[key], maps, core_ids=cores)
    out = np.concatenate([r["yT"].T for r in res.results], axis=0)
    return np.ascontiguousarray(out[None]).astype(np.float32)


FUSED = False
DBG = 9


def kernel(**inputs):
    return run(Cfg(), inputs, fused=FUSED)
```
